# Optimizing a Trainium2 kernel written in Bass

```python
import math
import jax, jax.numpy as jnp
from jax import lax
import numpy as np

D_MODEL = 1024
BATCH = 4
SEQ = 8192
DEPTH = 1

D_FF = 2816
SSM_WIDTH = 512
SSM_GROUP = 16
SSM_GROUPS = SSM_WIDTH // SSM_GROUP
SSM_STATE = 64
DT_MIN = 1e-3
DT_MAX = 1e-1
ATT_HEADS = 8
ATT_HEAD_DIM = 64
ATT_WIDTH = ATT_HEADS * ATT_HEAD_DIM
GRID_W = 64
WIN_H = 8
WIN_W = 16
IN_COLS = SSM_WIDTH + 3 * ATT_WIDTH + 2 * D_MODEL
SPLITS = (SSM_WIDTH,
          SSM_WIDTH + ATT_WIDTH,
          SSM_WIDTH + 2 * ATT_WIDTH,
          SSM_WIDTH + 3 * ATT_WIDTH,
          SSM_WIDTH + 3 * ATT_WIDTH + D_MODEL)
EPS = 1e-6
NEG_INF = -1e30

kernel_name = "hybrid_s5_natten_macaron_encoder"


def rms_norm(x, gain):
    xf = x.astype(jnp.float32)
    inv = lax.rsqrt(jnp.mean(xf * xf, axis=-1, keepdims=True) + EPS)
    return (xf * inv * gain.astype(jnp.float32)).astype(x.dtype)


def swiglu(x, w_gate, w_up, w_down):
    return (jax.nn.silu(x @ w_gate) * (x @ w_up)) @ w_down


def _complex_scan_combine(left, right):
    a1r, a1i, b1r, b1i = left
    a2r, a2i, b2r, b2i = right
    return (a1r * a2r - a1i * a2i,
            a1r * a2i + a1i * a2r,
            a2r * b1r - a2i * b1i + b2r,
            a2r * b1i + a2i * b1r + b2i)


def s5_direction(u, a_re, a_im, log_dt, b_re, b_im, c_re, c_im, reverse):
    f32 = jnp.float32
    dt = jnp.exp(log_dt.astype(f32))[:, None]
    lam_re = a_re.astype(f32)
    lam_im = a_im.astype(f32)
    zr, zi = lam_re * dt, lam_im * dt
    mag = jnp.exp(zr)
    lb_re, lb_im = mag * jnp.cos(zi), mag * jnp.sin(zi)
    den = lam_re * lam_re + lam_im * lam_im
    nr, ni = lb_re - 1.0, lb_im
    f_re = (nr * lam_re + ni * lam_im) / den
    f_im = (ni * lam_re - nr * lam_im) / den
    br, bi = b_re.astype(f32), b_im.astype(f32)
    bb_re = f_re[..., None] * br - f_im[..., None] * bi
    bb_im = f_re[..., None] * bi + f_im[..., None] * br
    uf = u.astype(f32)
    bu_re = jnp.einsum('blgc,gpc->blgp', uf, bb_re)
    bu_im = jnp.einsum('blgc,gpc->blgp', uf, bb_im)
    seq_len = u.shape[1]
    a_shape = (1, seq_len) + lb_re.shape
    a_seq_re = jnp.broadcast_to(lb_re, a_shape)
    a_seq_im = jnp.broadcast_to(lb_im, a_shape)
    _, _, s_re, s_im = lax.associative_scan(
        _complex_scan_combine, (a_seq_re, a_seq_im, bu_re, bu_im),
        reverse=reverse, axis=1)
    return (jnp.einsum('blgp,gcp->blgc', s_re, c_re.astype(f32))
            - jnp.einsum('blgp,gcp->blgc', s_im, c_im.astype(f32)))


def neighbourhood_attention_2d(q, k, v, rpb):
    bsz, seq_len, _ = q.shape
    rows = seq_len // GRID_W
    kh = min(WIN_H, rows)
    grid = (bsz, rows, GRID_W, ATT_HEADS, ATT_HEAD_DIM)
    qg = q.reshape(grid) * (ATT_HEAD_DIM ** -0.5)
    kg = k.reshape(grid)
    vg = v.reshape(grid)
    r = jnp.arange(rows)
    row_start = jnp.clip(r - kh // 2, 0, rows - kh)
    row_idx = row_start[:, None] + jnp.arange(kh)[None, :]
    k_band = kg[:, row_idx]
    v_band = vg[:, row_idx]
    col = jnp.arange(GRID_W)
    col_start = jnp.clip(col - WIN_W // 2, 0, GRID_W - WIN_W)
    col_mask = ((col[None, :] >= col_start[:, None])
                & (col[None, :] < col_start[:, None] + WIN_W))
    scores = jnp.einsum('brqhd,brkchd->bhrqkc', qg, k_band).astype(jnp.float32)
    dr = row_idx - r[:, None] + (WIN_H - 1)
    dc = jnp.clip(col[None, :] - col[:, None], -(WIN_W - 1), WIN_W - 1) + (WIN_W - 1)
    bias = rpb.astype(jnp.float32)[:, dr[:, None, :, None], dc[None, :, None, :]]
    scores = jnp.where(col_mask[:, None, :], scores + bias[None], NEG_INF)
    probs = jax.nn.softmax(scores, axis=(-2, -1)).astype(v.dtype)
    out = jnp.einsum('bhrqkc,brkchd->brqhd', probs, v_band)
    return out.reshape(bsz, seq_len, ATT_WIDTH)


def setup_inputs(seed: int = 0) -> dict:
    key = jax.random.key(seed)
    ks = iter(jax.random.split(key, 40))
    f32 = jnp.float32

    def nrm(shape, scale):
        return jax.random.normal(next(ks), shape, f32) * scale

    def gain(shape):
        return 1.0 + nrm(shape, 0.01)

    L_ = DEPTH
    G, P, C = SSM_GROUPS, SSM_STATE, SSM_GROUP
    a_im_init = jnp.broadcast_to(jnp.pi * jnp.arange(P, dtype=f32), (L_, G, P))

    def log_dt():
        return jax.random.uniform(next(ks), (L_, G), f32,
                                  minval=math.log(DT_MIN), maxval=math.log(DT_MAX))

    inp = {}
    inp["x"] = nrm((BATCH, SEQ, D_MODEL), 1.0)
    inp["ffn1_norm"] = gain((L_, D_MODEL))
    inp["ffn1_w_gate"] = nrm((L_, D_MODEL, D_FF), D_MODEL ** -0.5)
    inp["ffn1_w_up"] = nrm((L_, D_MODEL, D_FF), D_MODEL ** -0.5)
    inp["ffn1_w_down"] = nrm((L_, D_FF, D_MODEL), D_FF ** -0.5)
    inp["mix_norm"] = gain((L_, D_MODEL))
    inp["w_in"] = nrm((L_, D_MODEL, IN_COLS), D_MODEL ** -0.5)
    inp["ssm_a_re_fwd"] = -0.5 + nrm((L_, G, P), 0.01)
    inp["ssm_a_im_fwd"] = a_im_init + nrm((L_, G, P), 0.01)
    inp["ssm_log_dt_fwd"] = log_dt()
    inp["ssm_b_re_fwd"] = nrm((L_, G, P, C), (2.0 * C) ** -0.5)
    inp["ssm_b_im_fwd"] = nrm((L_, G, P, C), (2.0 * C) ** -0.5)
    inp["ssm_c_re_fwd"] = nrm((L_, G, C, P), P ** -0.5)
    inp["ssm_c_im_fwd"] = nrm((L_, G, C, P), P ** -0.5)
    inp["ssm_a_re_bwd"] = -0.5 + nrm((L_, G, P), 0.01)
    inp["ssm_a_im_bwd"] = a_im_init + nrm((L_, G, P), 0.01)
    inp["ssm_log_dt_bwd"] = log_dt()
    inp["ssm_b_re_bwd"] = nrm((L_, G, P, C), (2.0 * C) ** -0.5)
    inp["ssm_b_im_bwd"] = nrm((L_, G, P, C), (2.0 * C) ** -0.5)
    inp["ssm_c_re_bwd"] = nrm((L_, G, C, P), P ** -0.5)
    inp["ssm_c_im_bwd"] = nrm((L_, G, C, P), P ** -0.5)
    inp["ssm_d"] = nrm((L_, SSM_WIDTH), 1.0)
    inp["ssm_w_glu"] = nrm((L_, SSM_WIDTH, SSM_WIDTH), SSM_WIDTH ** -0.5)
    inp["ssm_b_glu"] = nrm((L_, SSM_WIDTH), 0.02)
    inp["att_rpb"] = nrm((L_, ATT_HEADS, 2 * WIN_H - 1, 2 * WIN_W - 1), 0.02)
    inp["w_branch_ssm"] = nrm((L_, SSM_WIDTH, D_MODEL), SSM_WIDTH ** -0.5)
    inp["w_branch_att"] = nrm((L_, ATT_WIDTH, D_MODEL), ATT_WIDTH ** -0.5)
    inp["w_out"] = nrm((L_, D_MODEL, D_MODEL), D_MODEL ** -0.5)
    inp["ffn2_norm"] = gain((L_, D_MODEL))
    inp["ffn2_w_gate"] = nrm((L_, D_MODEL, D_FF), D_MODEL ** -0.5)
    inp["ffn2_w_up"] = nrm((L_, D_MODEL, D_FF), D_MODEL ** -0.5)
    inp["ffn2_w_down"] = nrm((L_, D_FF, D_MODEL), D_FF ** -0.5)
    inp["final_norm"] = gain((D_MODEL,))
    return inp


def reference(x, ffn1_norm, ffn1_w_gate, ffn1_w_up, ffn1_w_down, mix_norm, w_in,
              ssm_a_re_fwd, ssm_a_im_fwd, ssm_log_dt_fwd, ssm_b_re_fwd, ssm_b_im_fwd,
              ssm_c_re_fwd, ssm_c_im_fwd,
              ssm_a_re_bwd, ssm_a_im_bwd, ssm_log_dt_bwd, ssm_b_re_bwd, ssm_b_im_bwd,
              ssm_c_re_bwd, ssm_c_im_bwd,
              ssm_d, ssm_w_glu, ssm_b_glu, att_rpb, w_branch_ssm, w_branch_att, w_out,
              ffn2_norm, ffn2_w_gate, ffn2_w_up, ffn2_w_down, final_norm):
    bsz, seq_len, _ = x.shape
    h = x
    for layer in range(DEPTH):
        h = h + 0.5 * swiglu(rms_norm(h, ffn1_norm[layer]), ffn1_w_gate[layer],
                             ffn1_w_up[layer], ffn1_w_down[layer])
        u = rms_norm(h, mix_norm[layer])
        z = u @ w_in[layer]
        z_ssm, z_q, z_k, z_v, g_ssm, g_att = jnp.split(z, SPLITS, axis=-1)

        us = z_ssm.reshape(bsz, seq_len, SSM_GROUPS, SSM_GROUP)
        y_fwd = s5_direction(us, ssm_a_re_fwd[layer], ssm_a_im_fwd[layer], ssm_log_dt_fwd[layer],
                             ssm_b_re_fwd[layer], ssm_b_im_fwd[layer],
                             ssm_c_re_fwd[layer], ssm_c_im_fwd[layer], False)
        y_bwd = s5_direction(us, ssm_a_re_bwd[layer], ssm_a_im_bwd[layer], ssm_log_dt_bwd[layer],
                             ssm_b_re_bwd[layer], ssm_b_im_bwd[layer],
                             ssm_c_re_bwd[layer], ssm_c_im_bwd[layer], True)
        y_s = (y_fwd + y_bwd).reshape(bsz, seq_len, SSM_WIDTH) \
            + ssm_d[layer].astype(jnp.float32) * z_ssm.astype(jnp.float32)
        y_s = jax.nn.gelu(y_s.astype(x.dtype))
        y_s = y_s * jax.nn.sigmoid(y_s @ ssm_w_glu[layer] + ssm_b_glu[layer])
        branch_ssm = y_s @ w_branch_ssm[layer]

        y_a = neighbourhood_attention_2d(z_q, z_k, z_v, att_rpb[layer])
        branch_att = y_a @ w_branch_att[layer]

        merged = jax.nn.sigmoid(g_ssm) * branch_ssm + jax.nn.sigmoid(g_att) * branch_att
        h = h + merged @ w_out[layer]

        h = h + 0.5 * swiglu(rms_norm(h, ffn2_norm[layer]), ffn2_w_gate[layer],
                             ffn2_w_up[layer], ffn2_w_down[layer])
    return rms_norm(h, final_norm)
```

```python
import os
import numpy as np
import ml_dtypes
from contextlib import ExitStack
import concourse.bass as bass
import concourse.mybir as mybir
from concourse.bass_utils import run_bass_kernel_spmd

F32 = mybir.dt.float32
BF16 = mybir.dt.bfloat16
AF = mybir.ActivationFunctionType
ALU = mybir.AluOpType

NCORES = 8
D = 1024
DFF = 2816
NF = 22
SEQ = 8192
TOK = 4096
HALO = 256
EXT = TOK + 2 * HALO
EPS = 1e-6
NEG = -30000.0
DEBUG = bool(int(os.environ.get("MK_DEBUG", "0")))
STOP = int(os.environ.get("MK_STOP", "9"))

ENGS = ["pe", "act", "dve", "pool", "sp"]
SELF_SYNC = {"act", "dve", "pool"}


class Prog:
    def __init__(self, nc, stack):
        self.nc = nc
        self.stack = stack
        self.sems = {}
        self.semstack = stack
        self.reset()
        self.dma_cnt = {}
        self.eng_total = {e: 0 for e in ENGS}

    def reset(self):
        self.eng_ops = {e: [] for e in ENGS}
        self.lastw = {}
        self.readers = {}
        self.dma_toks = []

    def sbuf(self, name, shape, dt):
        return self.stack.enter_context(self.nc.sbuf_tensor("sb_" + name, list(shape), dt))

    def sem(self, name):
        if name not in self.sems:
            self.sems[name] = self.semstack.enter_context(self.nc.semaphore(name))
        return self.sems[name]

    def op(self, eng, fn, reads=(), writes=(), dma=None, inc=16):
        idx = len(self.eng_ops[eng])
        deps = set()
        for r in reads:
            t = self.lastw.get(r)
            if t is not None:
                deps.add(t)
        for w in writes:
            t = self.lastw.get(w)
            if t is not None:
                deps.add(t)
            for rd in self.readers.get(w, ()):
                deps.add(rd)
        if dma is not None:
            cnt = self.dma_cnt.get(dma, 0) + inc
            self.dma_cnt[dma] = cnt
            tok = ("dma", dma, cnt)
            self.dma_toks.append(tok)
        else:
            tok = ("eng", eng, idx)
        deps.discard(tok)
        if eng not in SELF_SYNC:
            deps = {d for d in deps if not (d[0] == "eng" and d[1] == eng)}
        self.eng_ops[eng].append(dict(fn=fn, deps=deps, tok=tok, dma=dma, inc=inc))
        for r in reads:
            self.readers.setdefault(r, []).append(tok)
        for w in writes:
            self.lastw[w] = tok
            self.readers[w] = []
        return tok

    def emit_phase(self, name):
        nc = self.nc
        last = {}
        for t in self.dma_toks:
            last[t[1]] = max(last.get(t[1], 0), t[2])
        self.eng_ops["sp"].append(dict(fn=None, deps={("dma", k, v) for k, v in last.items()}, tok=None, dma=None, inc=0))
        mile = {e: set() for e in ENGS}
        for e in ENGS:
            for rec in self.eng_ops[e]:
                for d in rec["deps"]:
                    if d[0] == "eng":
                        mile[d[1]].add(d[2])
        mcount = {}
        base = dict(self.eng_total)
        for e in ENGS:
            c = base[e]
            for i in range(len(self.eng_ops[e])):
                if i in mile[e]:
                    c += 1
                    mcount[(e, i)] = c
            self.eng_total[e] = c
        bname = dict(pe="tensor", act="scalar", dve="vector", pool="gpsimd", sp="sync")
        eng_ops = self.eng_ops
        with nc.Block() as block:
            for e in ENGS:
                if not eng_ops[e]:
                    continue

                def body(engobj, e=e):
                    waited = {}
                    for i, rec in enumerate(eng_ops[e]):
                        need = {}
                        for d in rec["deps"]:
                            if d[0] == "eng":
                                key = ("p", d[1]); val = mcount[(d[1], d[2])]; sem = self.sem("prog_" + d[1])
                            else:
                                key = ("d", d[1]); val = d[2]; sem = self.sem("dma_" + d[1])
                            if need.get(key, (None, 0))[1] < val:
                                need[key] = (sem, val)
                        for key, (sem, val) in need.items():
                            if waited.get(key, 0) < val:
                                engobj.wait_ge(sem, val)
                                waited[key] = val
                        if rec["fn"] is None:
                            continue
                        inst = rec["fn"](engobj)
                        if rec["dma"] is not None:
                            if rec["inc"] == 1:
                                inst.then_inc(self.sem("dma_" + rec["dma"]))
                            else:
                                inst.then_inc(self.sem("dma_" + rec["dma"]), rec["inc"])
                        elif (e, i) in mcount:
                            inst.then_inc(self.sem("prog_" + e), 1)

                getattr(block, bname[e])(body)
        self.reset()


def _global_tok(core, i):
    half = core % 2
    g = i if half == 0 else (SEQ - 1 - i)
    return g


def _bias_tables(core, rpb):
    half = core % 2
    combos = [(0, o) for o in (0, 1, 2, 3)] + [(1, o) for o in (-1, 0, 1, 2)] + [(10, o) for o in (-2, -1, 0, 1, 2)]
    out = np.full((128, 13, 8, 128), NEG, np.float32)
    kr = np.arange(128) // 64
    kc = np.arange(128) % 64
    for ti, (mq, o) in enumerate(combos):
        mk = mq + o
        rq_l = 2 * mq + kr
        cq_l = kc
        rk_l = 2 * mk + kr
        ck_l = kc
        if half == 0:
            rq, cq, rk, ck = rq_l, cq_l, rk_l, ck_l
        else:
            rq, cq, rk, ck = 127 - rq_l, 63 - cq_l, 127 - rk_l, 63 - ck_l
        RQ = rq[None, :]; CQ = cq[None, :]; RK = rk[:, None]; CK = ck[:, None]
        rs = np.clip(RQ - 4, 0, 120)
        cs = np.clip(CQ - 8, 0, 48)
        valid = (RK >= 0) & (RK < 128) & (RK >= rs) & (RK < rs + 8) & (CK >= cs) & (CK < cs + 16)
        dr = np.clip(RK - RQ + 7, 0, 14)
        dc = np.clip(CK - CQ, -15, 15) + 15
        dr = np.broadcast_to(dr, (128, 128)); dc = np.broadcast_to(dc, (128, 128))
        for h in range(8):
            g = rpb[h][dr, dc]
            out[:, ti, h, :] = np.where(valid, g, np.float32(NEG))
    return out.reshape(128, 104, 128)


def _prep_core(core, inp):
    b = core // 2
    half = core % 2
    x = inp["x"][b]
    idx = np.arange(-HALO, TOK + HALO)
    g = idx if half == 0 else (SEQ - 1 - idx)
    if half == 1:
        g = g - 0
    gl = np.where(half == 0, idx, SEQ - 1 - idx)
    valid = (gl >= 0) & (gl < SEQ)
    xe = np.zeros((EXT, D), np.float32)
    xe[valid] = x[gl[valid]]
    m = {"xT": np.ascontiguousarray(xe.T)}
    A, B = ("fwd", "bwd") if half == 0 else ("bwd", "fwd")

    def lay_gp(a):
        return a.reshape(2, 16, 64).transpose(0, 2, 1).reshape(128, 16)

    cols = []
    for nm in ("ssm_a_re_", "ssm_a_im_"):
        cols.append(np.stack([lay_gp(inp[nm + A][0]), lay_gp(inp[nm + B][0])], 1).reshape(128, 32))
    ld = []
    for dname in (A, B):
        l = inp["ssm_log_dt_" + dname][0]
        ld.append(lay_gp(np.repeat(l[:, None], 64, 1)))
    cols.append(np.stack(ld, 1).reshape(128, 32))
    for nm in ("ssm_b_re_", "ssm_b_im_"):
        t = []
        for dname in (A, B):
            a = inp[nm + dname][0]
            t.append(a.reshape(2, 16, 64, 16).transpose(0, 2, 1, 3).reshape(128, 16, 16))
        cols.append(np.stack(t, 1).reshape(128, 512))
    for nm in ("ssm_c_re_", "ssm_c_im_"):
        t = []
        for dname in (A, B):
            a = inp[nm + dname][0]
            t.append(a.reshape(2, 16, 16, 64).transpose(0, 3, 1, 2).reshape(128, 16, 16))
        cols.append(np.stack(t, 1).reshape(128, 512))
    m["spar"] = np.ascontiguousarray(np.concatenate(cols, 1).astype(np.float32))
    m["dtab"] = np.ascontiguousarray(np.broadcast_to(inp["ssm_d"][0][None, :], (128, 512)).astype(np.float32))
    gains = np.stack([inp["ffn1_norm"][0], inp["mix_norm"][0], inp["ffn2_norm"][0], inp["final_norm"]], 0)
    m["gains"] = np.ascontiguousarray(gains.reshape(4, 8, 128).transpose(2, 0, 1).reshape(128, 32))
    m["bglu"] = np.ascontiguousarray(inp["ssm_b_glu"][0].reshape(4, 128).T)
    m["btab"] = _bias_tables(core, inp["att_rpb"][0])
    sel = np.zeros((128, 8), np.float32)
    sel[:, core ^ 1] = 1.0
    m["sel"] = sel
    return m


def _consts():
    c = {}
    c["ident"] = np.eye(128, dtype=np.float32)
    tp = np.arange(128) // 16
    s = np.arange(512) // 16
    c["maskA"] = (s[None, :] >= tp[:, None]).astype(np.float32)
    c["maskB"] = (s[None, :] <= tp[:, None] + 24).astype(np.float32)
    return c


SHARED = ["ffn1_w_gate", "ffn1_w_up", "ffn1_w_down", "w_in", "ssm_w_glu", "w_branch_ssm", "w_branch_att",
          "w_out", "ffn2_w_gate", "ffn2_w_up", "ffn2_w_down"]


def build():
    nc = bass.Bass("TRN2", target_bir_lowering=False)
    EI = "ExternalInput"
    dk = "ExternalOutput" if DEBUG else None

    def dram(name, shape, dt, kind=None):
        if kind is None:
            return nc.dram_tensor(name, list(shape), dt)
        return nc.dram_tensor(name, list(shape), dt, kind=kind)

    xT = dram("xT", [D, EXT], F32, EI)
    spar = dram("spar", [128, 2144], F32, EI)
    dtab_d = dram("dtab", [128, 512], F32, EI)
    gains_d = dram("gains", [128, 32], F32, EI)
    bglu_d = dram("bglu", [128, 4], F32, EI)
    btab_d = dram("btab", [128, 104, 128], F32, EI)
    sel_d = dram("sel", [128, 8], F32, EI)
    ident_d = dram("ident", [128, 128], F32, EI)
    maskA_d = dram("maskA", [128, 512], F32, EI)
    maskB_d = dram("maskB", [128, 512], F32, EI)
    W = {}
    W["ffn1_w_gate"] = dram("ffn1_w_gate", [D, DFF], F32, EI)
    W["ffn1_w_up"] = dram("ffn1_w_up", [D, DFF], F32, EI)
    W["ffn1_w_down"] = dram("ffn1_w_down", [DFF, D], F32, EI)
    W["ffn2_w_gate"] = dram("ffn2_w_gate", [D, DFF], F32, EI)
    W["ffn2_w_up"] = dram("ffn2_w_up", [D, DFF], F32, EI)
    W["ffn2_w_down"] = dram("ffn2_w_down", [DFF, D], F32, EI)
    W["w_in"] = dram("w_in", [D, 4096], F32, EI)
    W["ssm_w_glu"] = dram("ssm_w_glu", [512, 512], F32, EI)
    W["w_branch_ssm"] = dram("w_branch_ssm", [512, D], F32, EI)
    W["w_branch_att"] = dram("w_branch_att", [512, D], F32, EI)
    W["w_out"] = dram("w_out", [D, D], F32, EI)
    outT = dram("outT", [D, TOK], F32, "ExternalOutput")

    h1_s = dram("h1_s", [8, 128, EXT], F32, dk)
    qT_s = dram("qT_s", [4, 128, EXT], BF16, dk)
    kT_s = dram("kT_s", [4, 128, EXT], BF16, dk)
    va_s = dram("va_s", [EXT, 520], BF16, dk)
    zs_s = dram("zs_s", [EXT, 512], BF16, dk)
    sgs_s = dram("sgs_s", [8, 128, EXT], BF16, dk)
    sga_s = dram("sga_s", [8, 128, EXT], BF16, dk)
    fin_in = dram("fin_in", [128, 32], F32)
    fin_all = dram("fin_all", [128 * NCORES, 32], F32)
    if DEBUG:
        dbg_ys = dram("dbg_ys", [4, 128, TOK], BF16, "ExternalOutput")
        dbg_ya = dram("dbg_ya", [4, 128, TOK], BF16, "ExternalOutput")
        dbg_h2 = dram("dbg_h2", [8, 128, TOK], F32, "ExternalOutput")
        dbg_ss = dram("dbg_ss", [128, 2 * 2 * 16 * 129], F32, "ExternalOutput")

    with ExitStack() as outer:
        P = Prog(nc, outer)
        ps = outer.enter_context(nc.psum_tensor("ps", [128, 8, 512], F32))
        psflat = ps[:].rearrange("p a b -> p (a b)")

        def bank(i):
            return ps[:, i, :]

        def bk(i):
            return "pb%d" % i

        ident = P.sbuf("ident", [128, 128], BF16)
        gains = P.sbuf("gains", [128, 4, 8], F32)
        ones_bf = P.sbuf("ones_bf", [128, 128], BF16)
        P.op("pool", lambda e: e.dma_start(out=ident[:], in_=ident_d.ap()), writes=["ident"], dma="ident")
        P.op("sp", lambda e: e.dma_start(out=gains[:].rearrange("p a b -> p (a b)"), in_=gains_d.ap()), writes=["gains"], dma="gains")
        P.op("dve", lambda e: e.memset(ones_bf[:], 1.0), writes=["ones"])

        def rmsnorm(h, u, sq, sd, rstd, gidx, s, ncols=512):
            sl = slice(s * 512, s * 512 + 512)
            P.op("act", lambda e: e.activation(out=sq[:], in_=h[:, :, sl], func=AF.Square), reads=["h"], writes=["sq"])
            for c in range(8):
                P.op("pe", lambda e, c=c: e.matmul(bank(0), lhsT=ones_bf[:], rhs=sq[:, c, :], start=(c == 0), stop=(c == 7)),
                     reads=["sq", "ones"], writes=[bk(0)])
            P.op("act", lambda e: e.activation(out=sd[:], in_=bank(0), func=AF.Sqrt, scale=1.0 / D, bias=epsb[:]),
                 reads=[bk(0), "epsb"], writes=["sd"])
            P.op("dve", lambda e: e.reciprocal(out=rstd[:], in_=sd[:]), reads=["sd"], writes=["rstd"])
            return sl

        def apply_norm(h, u, rstd, gidx, sl, ukey="u"):
            s_ = sl.start // 512
            for c in range(8):
                P.op("dve", lambda e, c=c: e.scalar_tensor_tensor(out=u[:, c, sl], in0=h[:, c, sl], scalar=gains[:, gidx, c:c + 1],
                                                                  in1=rstd[:], op0=ALU.mult, op1=ALU.mult),
                     reads=["h", "rstd", "gains"], writes=["u%d_%d" % (s_, c)])

        def ffn(h, u, act, wgate, wup, wdown, NS, bufs):
            wg2, wu2, wdq, sgb = bufs
            WGC = wg2[0].shape[2]
            NFF = WGC // 128
            for fb in range(DFF // WGC):
                b = fb % 2
                cs = slice(fb * WGC, fb * WGC + WGC)
                P.op("pool", lambda e, b=b, cs=cs: e.dma_start(out=wg2[b][:], in_=wgate.ap().rearrange("(c p) f -> p c f", p=128)[:, :, cs]),
                     writes=["wg%d" % b], dma="wg%d" % b)
                P.op("pool", lambda e, b=b, cs=cs: e.dma_start(out=wu2[b][:], in_=wup.ap().rearrange("(c p) f -> p c f", p=128)[:, :, cs]),
                     writes=["wu%d" % b], dma="wu%d" % b)
                for ff in range(NFF):
                    f = fb * NFF + ff
                    for s in range(NS):
                        sl = slice(s * 512, s * 512 + 512)
                        for c in range(8):
                            P.op("pe", lambda e, b=b, c=c, ff=ff, sl=sl, s=s: e.matmul(bank(1 + s), lhsT=wg2[b][:, c, ff * 128:(ff + 1) * 128], rhs=u[:, c, sl],
                                                                                       start=(c == 0), stop=(c == 7)),
                                 reads=["wg%d" % b, "u%d_%d" % (s, c)], writes=[bk(1 + s)])
                        for c in range(8):
                            P.op("pe", lambda e, b=b, c=c, ff=ff, sl=sl, s=s: e.matmul(bank(3 + s), lhsT=wu2[b][:, c, ff * 128:(ff + 1) * 128], rhs=u[:, c, sl],
                                                                                       start=(c == 0), stop=(c == 7)),
                                 reads=["wu%d" % b, "u%d_%d" % (s, c)], writes=[bk(3 + s)])
                        P.op("act", lambda e, s=s: e.activation(out=sgb[s][:], in_=bank(1 + s), func=AF.Silu), reads=[bk(1 + s)], writes=["sgb%d" % s])
                        P.op("dve", lambda e, s=s, f=f, sl=sl: e.tensor_tensor(out=act[:, f, sl], in0=sgb[s][:], in1=bank(3 + s), op=ALU.mult),
                             reads=["sgb%d" % s, bk(3 + s)], writes=["act"])
            WDC = wdq[0].shape[2]
            for oq in range(1024 // WDC):
                b = oq % 2
                P.op("pool", lambda e, b=b, oq=oq: e.dma_start(out=wdq[b][:], in_=wdown.ap().rearrange("(f p) o -> p f o", p=128)[:, :, oq * WDC:(oq + 1) * WDC]),
                     writes=["wd%d" % b], dma="wd%d" % b)
                for oo in range(WDC // 128):
                    o = oq * (WDC // 128) + oo
                    for s in range(NS):
                        sl = slice(s * 512, s * 512 + 512)
                        for f in range(NF):
                            P.op("pe", lambda e, b=b, f=f, oo=oo, sl=sl, s=s: e.matmul(bank(5 + s), lhsT=wdq[b][:, f, oo * 128:(oo + 1) * 128], rhs=act[:, f, sl],
                                                                                       start=(f == 0), stop=(f == NF - 1)),
                                 reads=["wd%d" % b, "act"], writes=[bk(5 + s)])
                        P.op("dve", lambda e, o=o, sl=sl, s=s: e.scalar_tensor_tensor(out=h[:, o, sl], in0=bank(5 + s), scalar=0.5, in1=h[:, o, sl],
                                                                                      op0=ALU.mult, op1=ALU.add),
                             reads=[bk(5 + s), "h"], writes=["h"])

        epsb = P.sbuf("epsb", [128, 1], F32)
        halfpi = P.sbuf("halfpi", [128, 1], F32)
        P.op("dve", lambda e: e.memset(epsb[:], EPS), writes=["epsb"])
        P.op("dve", lambda e: e.memset(halfpi[:], float(np.pi / 2)), writes=["halfpi"])

        with ExitStack() as ph1:
            P.stack = ph1
            h = P.sbuf("h", [128, 8, 1024], F32)
            u = P.sbuf("u", [128, 8, 1024], BF16)
            act = P.sbuf("act", [128, NF, 1024], BF16)
            sq = P.sbuf("sq", [128, 8, 512], BF16)
            sd = P.sbuf("sd", [128, 512], F32)
            rstd = P.sbuf("rstd", [128, 512], F32)
            wg2 = [P.sbuf("wg2_%d" % i, [128, 8, 256], BF16) for i in range(2)]
            wu2 = [P.sbuf("wu2_%d" % i, [128, 8, 256], BF16) for i in range(2)]
            wdq = [P.sbuf("wdq_%d" % i, [128, NF, 256], BF16) for i in range(2)]
            sgb = [P.sbuf("sgb_%d" % i, [128, 512], BF16) for i in range(2)]
            win = [P.sbuf("win_%d" % i, [128, 8, 512], BF16) for i in range(2)]
            stg = [P.sbuf("stg_%d" % i, [128, 4, 1024], BF16) for i in range(2)]
            stz = [P.sbuf("stz_%d" % i, [128, 512], BF16) for i in range(2)]
            stv = [P.sbuf("stv_%d" % i, [128, 8, 65], BF16) for i in range(2)]
            for i in range(2):
                P.op("dve", lambda e, i=i: e.memset(stv[i][:], 1.0), writes=["stv%d" % i])
            win_r = W["w_in"].ap().rearrange("(c p) f -> p c f", p=128)
            nwl = [0]

            def load_win(c0):
                b = nwl[0] % 2
                nwl[0] += 1
                P.op("pool", lambda e, b=b, c0=c0: e.dma_start(out=win[b][:], in_=win_r[:, :, c0:c0 + 512]), writes=["win%d" % b], dma="win%d" % b)
                return b

            groups = [(0, 2), (1024, 2), (2048, 2), (3072, 2), (4096, 1)]
            nstg = [0]
            ntm = [0]
            for (c0, NS) in groups:
                ncol = NS * 512
                P.op("sp", lambda e, c0=c0, ncol=ncol: e.dma_start(out=h[:, :, 0:ncol], in_=xT.ap().rearrange("(c p) t -> p c t", p=128)[:, :, c0:c0 + ncol]),
                     writes=["h"], dma="h")
                for s in range(NS):
                    sl = rmsnorm(h, u, sq, sd, rstd, 0, s)
                    apply_norm(h, u, rstd, 0, sl)
                ffn(h, u, act, W["ffn1_w_gate"], W["ffn1_w_up"], W["ffn1_w_down"], NS, (wg2, wu2, wdq, sgb))
                P.op("sp", lambda e, c0=c0, ncol=ncol: e.dma_start(out=h1_s.ap().rearrange("c p t -> p c t")[:, :, c0:c0 + ncol], in_=h[:, :, 0:ncol]),
                     reads=["h"], writes=["d_h1"], dma="h1o")
                for s in range(NS):
                    sl = rmsnorm(h, u, sq, sd, rstd, 1, s)
                    apply_norm(h, u, rstd, 1, sl)
                for (wc0, dst, kind) in [(512, qT_s, "q"), (1024, kT_s, "k"), (2048, sgs_s, "g0"), (2560, sgs_s, "g1"), (3072, sga_s, "g0"), (3584, sga_s, "g1")]:
                    b = load_win(wc0)
                    sb = nstg[0] % 2
                    nstg[0] += 1
                    for cb in range(4):
                        for s in range(NS):
                            sl = slice(s * 512, s * 512 + 512)
                            bn = 7 if (cb + s) % 2 else 0
                            for c in range(8):
                                P.op("pe", lambda e, b=b, c=c, cb=cb, sl=sl, bn=bn: e.matmul(bank(bn), lhsT=win[b][:, c, cb * 128:(cb + 1) * 128], rhs=u[:, c, sl],
                                                                                      start=(c == 0), stop=(c == 7)),
                                     reads=["win%d" % b, "u%d_%d" % (s, c)], writes=[bk(bn)])
                            if kind == "q":
                                P.op("act", lambda e, sb=sb, cb=cb, sl=sl, bn=bn: e.activation(out=stg[sb][:, cb, sl], in_=bank(bn), func=AF.Copy, scale=0.125),
                                     reads=[bk(bn)], writes=["stg%d" % sb])
                            elif kind == "k":
                                P.op("act", lambda e, sb=sb, cb=cb, sl=sl, bn=bn: e.activation(out=stg[sb][:, cb, sl], in_=bank(bn), func=AF.Copy),
                                     reads=[bk(bn)], writes=["stg%d" % sb])
                            else:
                                P.op("act", lambda e, sb=sb, cb=cb, sl=sl, bn=bn: e.activation(out=stg[sb][:, cb, sl], in_=bank(bn), func=AF.Sigmoid),
                                     reads=[bk(bn)], writes=["stg%d" % sb])
                    r0 = 4 if kind == "g1" else 0
                    P.op("sp", lambda e, sb=sb, dst=dst, r0=r0, c0=c0, ncol=ncol: e.dma_start(
                        out=dst.ap().rearrange("c p t -> p c t")[:, r0:r0 + 4, c0:c0 + ncol], in_=stg[sb][:, :, 0:ncol]),
                        reads=["stg%d" % sb], writes=["d_" + dst.name + kind], dma="stgo%d" % sb)
                for (wc0, kind) in [(1536, "v"), (0, "z")]:
                    b = load_win(wc0)
                    for tb in range(NS * 4):
                        tsl = slice(tb * 128, tb * 128 + 128)
                        bn = 7 if tb % 2 else 0
                        for c in range(8):
                            P.op("pe", lambda e, b=b, c=c, tsl=tsl, bn=bn: e.matmul(bank(bn), lhsT=u[:, c, tsl], rhs=win[b][:, c, :], start=(c == 0), stop=(c == 7)),
                                 reads=["win%d" % b, "u%d_%d" % (tb // 4, c)], writes=[bk(bn)])
                        tb2 = ntm[0] % 2
                        ntm[0] += 1
                        r0 = c0 + tb * 128
                        if kind == "v":
                            P.op("act", lambda e, tb2=tb2, bn=bn: e.activation(out=stv[tb2][:, :, 0:64], in_=bank(bn).rearrange("p (h d) -> p h d", h=8), func=AF.Copy),
                                 reads=[bk(bn)], writes=["stv%d" % tb2])
                            P.op("sp", lambda e, tb2=tb2, r0=r0: e.dma_start(out=va_s[r0:r0 + 128, :], in_=stv[tb2][:].rearrange("p h d -> p (h d)")),
                                 reads=["stv%d" % tb2], writes=["d_va"], dma="stvo%d" % tb2)
                        else:
                            P.op("act", lambda e, tb2=tb2, bn=bn: e.activation(out=stz[tb2][:], in_=bank(bn), func=AF.Copy), reads=[bk(bn)], writes=["stz%d" % tb2])
                            P.op("sp", lambda e, tb2=tb2, r0=r0: e.dma_start(out=zs_s[r0:r0 + 128, :], in_=stz[tb2][:]),
                                 reads=["stz%d" % tb2], writes=["d_zs"], dma="stzo%d" % tb2)
            P.emit_phase("pass1")
        P.stack = outer
        if STOP <= 1:
            return nc
        UY = P.sbuf("UY", [128, 4 * TOK], BF16)
        ysT = UY[:].rearrange("p (c t) -> p c t", c=4)

        with ExitStack() as phs:
            P.stack = phs
            Zc = P.sbuf("Zc", [128, 32, 512], BF16)
            U = UY[:].rearrange("p (g b j) -> p g b j", g=32, b=4)
            SSraw = P.sbuf("SSraw", [128, 2 * 2 * 16 * 129], F32)
            SS = SSraw[:].rearrange("p (a b c d) -> p a b c d", a=2, b=2, c=16)
            Zc2 = SSraw[:].bitcast(BF16)[:, 0:16384].rearrange("p (b g k) -> p b g k", b=4, g=32)
            swp = [P.sbuf("swp%d" % i, [128, 2, 16], F32) for i in range(2)]
            SSb = P.sbuf("SSb", [128, 2, 2, 16, 129], BF16)
            sp_t = P.sbuf("sp_t", [128, 2144], F32)
            dtab = P.sbuf("dtab", [128, 512], F32)
            maskA = P.sbuf("maskA", [128, 512], F32)
            maskB = P.sbuf("maskB", [128, 512], F32)
            sel = P.sbuf("sel", [128, 8], F32)
            finb = P.sbuf("finb", [128, 8, 32], F32)
            P.op("sp", lambda e: e.dma_start(out=Zc[:].rearrange("p t c -> p (t c)"), in_=zs_s[HALO:HALO + TOK, :].rearrange("(j t) c -> j (t c)", t=32)),
                 writes=["Zc"], dma="Zc")
            P.op("sp", lambda e: e.dma_start(out=sp_t[:], in_=spar.ap()), writes=["sp_t"], dma="sp_t")
            P.op("sp", lambda e: e.dma_start(out=dtab[:], in_=dtab_d.ap()), writes=["dtab"], dma="dtab")
            P.op("sp", lambda e: e.dma_start(out=maskA[:], in_=maskA_d.ap()), writes=["maskA"], dma="maskA")
            P.op("sp", lambda e: e.dma_start(out=maskB[:], in_=maskB_d.ap()), writes=["maskB"], dma="maskB")
            P.op("sp", lambda e: e.dma_start(out=sel[:], in_=sel_d.ap()), writes=["sel"], dma="sel")
            are = sp_t[:, 0:32]; aim = sp_t[:, 32:64]; ldt = sp_t[:, 64:96]
            Bre = sp_t[:, 96:608].rearrange("p (u c) -> p u c", c=16)
            Bim = sp_t[:, 608:1120].rearrange("p (u c) -> p u c", c=16)
            Cre = sp_t[:, 1120:1632].rearrange("p (u c) -> p u c", c=16)
            Cim = sp_t[:, 1632:2144].rearrange("p (u c) -> p u c", c=16)

            nt = [0]

            def T(shape, dt=F32):
                nt[0] += 1
                return P.sbuf("t%d" % nt[0], shape, dt)

            def V(fn, reads, writes, eng="dve"):
                P.op(eng, fn, reads=reads, writes=writes)

            def tt(o, a, b, op, r, w, eng="dve"):
                V(lambda e: e.tensor_tensor(out=o, in0=a, in1=b, op=op), r, w, eng)

            dt_ = T([128, 32]); zr = T([128, 32]); zi = T([128, 32]); e16 = T([128, 32]); sn = T([128, 32]); cs = T([128, 32])
            lre = T([128, 32]); lim = T([128, 32]); t1 = T([128, 32]); t2 = T([128, 32]); t3 = T([128, 32])
            V(lambda e: e.activation(out=dt_[:], in_=ldt, func=AF.Exp), ["sp_t"], ["dt"], "act")
            tt(zr[:], are, dt_[:], ALU.mult, ["sp_t", "dt"], ["zr"])
            tt(zi[:], aim, dt_[:], ALU.mult, ["sp_t", "dt"], ["zi"])
            V(lambda e: e.activation(out=e16[:], in_=zr[:], func=AF.Exp, scale=1.0 / 16), ["zr"], ["e16"], "act")
            V(lambda e: e.activation(out=sn[:], in_=zi[:], func=AF.Sin, scale=1.0 / 16), ["zi"], ["sn"], "act")
            V(lambda e: e.activation(out=cs[:], in_=zi[:], func=AF.Sin, scale=1.0 / 16, bias=halfpi[:]), ["zi", "halfpi"], ["cs"], "act")
            tt(lre[:], e16[:], cs[:], ALU.mult, ["e16", "cs"], ["lam"])
            tt(lim[:], e16[:], sn[:], ALU.mult, ["e16", "sn", "lam"], ["lam"])
            for _ in range(4):
                tt(t1[:], lre[:], lre[:], ALU.mult, ["lam"], ["t1"])
                tt(t2[:], lim[:], lim[:], ALU.mult, ["lam"], ["t2"])
                tt(t3[:], lre[:], lim[:], ALU.mult, ["lam"], ["t3"])
                tt(lre[:], t1[:], t2[:], ALU.subtract, ["t1", "t2", "lam"], ["lam"])
                V(lambda e: e.tensor_scalar(out=lim[:], in0=t3[:], scalar1=2.0, scalar2=None, op0=ALU.mult), ["t3", "lam"], ["lam"])
            e2 = T([128, 32]); lire = T([128, 32]); liim = T([128, 32])
            V(lambda e: e.activation(out=e2[:], in_=zr[:], func=AF.Exp, scale=-2.0), ["zr"], ["e2"], "act")
            tt(lire[:], lre[:], e2[:], ALU.mult, ["lam", "e2"], ["lami"])
            V(lambda e: e.scalar_tensor_tensor(out=liim[:], in0=lim[:], scalar=-1.0, in1=e2[:], op0=ALU.mult, op1=ALU.mult), ["lam", "e2", "lami"], ["lami"])
            den = T([128, 32]); nr = T([128, 32]); fre = T([128, 32]); fim = T([128, 32])
            tt(t1[:], are, are, ALU.mult, ["sp_t"], ["t1"])
            tt(t2[:], aim, aim, ALU.mult, ["sp_t"], ["t2"])
            tt(den[:], t1[:], t2[:], ALU.add, ["t1", "t2"], ["den"])
            V(lambda e: e.reciprocal(out=den[:], in_=den[:]), ["den"], ["den"])
            V(lambda e: e.tensor_scalar(out=nr[:], in0=lre[:], scalar1=-1.0, scalar2=None, op0=ALU.add), ["lam"], ["nr"])
            tt(t1[:], nr[:], are, ALU.mult, ["nr", "sp_t"], ["t1"])
            tt(t2[:], lim[:], aim, ALU.mult, ["lam", "sp_t"], ["t2"])
            tt(t1[:], t1[:], t2[:], ALU.add, ["t1", "t2"], ["t1"])
            tt(fre[:], t1[:], den[:], ALU.mult, ["t1", "den"], ["fre"])
            tt(t1[:], lim[:], are, ALU.mult, ["lam", "sp_t"], ["t1"])
            tt(t2[:], nr[:], aim, ALU.mult, ["nr", "sp_t"], ["t2"])
            tt(t1[:], t1[:], t2[:], ALU.subtract, ["t1", "t2"], ["t1"])
            tt(fim[:], t1[:], den[:], ALU.mult, ["t1", "den"], ["fim"])
            bbre = T([128, 32, 16]); bbim = T([128, 32, 16]); w1 = T([128, 32, 16]); w2 = T([128, 32, 16])
            freb = fre[:].unsqueeze(2).broadcast_to([128, 32, 16]); fimb = fim[:].unsqueeze(2).broadcast_to([128, 32, 16])
            tt(w1[:], Bre, freb, ALU.mult, ["sp_t", "fre"], ["w1"])
            tt(w2[:], Bim, fimb, ALU.mult, ["sp_t", "fim"], ["w2"])
            tt(bbre[:], w1[:], w2[:], ALU.subtract, ["w1", "w2"], ["bbre"])
            tt(w1[:], Bim, freb, ALU.mult, ["sp_t", "fre", "bbre"], ["w1"])
            tt(w2[:], Bre, fimb, ALU.mult, ["sp_t", "fim", "bbre"], ["w2"])
            tt(bbim[:], w1[:], w2[:], ALU.add, ["w1", "w2"], ["bbim"])

            def powtab(n, bre_, bim_, key):
                pr = T([128, 32, n]); pi = T([128, 32, n]); a1 = T([128, 32, n]); a2 = T([128, 32, n])
                V(lambda e: e.memset(pr[:, :, 0:1], 1.0), [], [key])
                V(lambda e: e.memset(pi[:, :, 0:1], 0.0), [key], [key])
                V(lambda e: e.tensor_copy(out=pr[:, :, 1:2], in_=bre_.unsqueeze(2)), ["lam", "lami", key], [key])
                V(lambda e: e.tensor_copy(out=pi[:, :, 1:2], in_=bim_.unsqueeze(2)), ["lam", "lami", key], [key])
                m = 1
                while m + 1 < n:
                    cnt = min(m, n - 1 - m)
                    src = slice(1, 1 + cnt); dst = slice(m + 1, m + 1 + cnt)
                    mr = pr[:, :, m:m + 1].broadcast_to([128, 32, cnt]); mi = pi[:, :, m:m + 1].broadcast_to([128, 32, cnt])
                    tt(a1[:, :, 0:cnt], pr[:, :, src], mr, ALU.mult, [key], [key + "a1"])
                    tt(a2[:, :, 0:cnt], pi[:, :, src], mi, ALU.mult, [key], [key + "a2"])
                    tt(pr[:, :, dst], a1[:, :, 0:cnt], a2[:, :, 0:cnt], ALU.subtract, [key + "a1", key + "a2", key], [key])
                    tt(a1[:, :, 0:cnt], pr[:, :, src], mi, ALU.mult, [key], [key + "a1"])
                    tt(a2[:, :, 0:cnt], pi[:, :, src], mr, ALU.mult, [key], [key + "a2"])
                    tt(pi[:, :, dst], a1[:, :, 0:cnt], a2[:, :, 0:cnt], ALU.add, [key + "a1", key + "a2", key], [key])
                    m = m + cnt
                return pr, pi

            pwr_, pwi_ = powtab(33, lre[:], lim[:], "pw")
            pnr_, pni_ = powtab(8, lire[:], liim[:], "pn")
            TPr = T([128, 32, 33]); TPi = T([128, 32, 33]); TNr = T([128, 32, 8]); TNi = T([128, 32, 8])
            for (dst, src, key, n) in [(TPr, pwr_, "pw", 33), (TPi, pwi_, "pw", 33), (TNr, pnr_, "pn", 8), (TNi, pni_, "pn", 8)]:
                V(lambda e, dst=dst, src=src: e.tensor_copy(out=dst[:, 0:16, :], in_=src[:, 0:16, :]), [key], ["TP"])
                V(lambda e, dst=dst, src=src: e.tensor_copy(out=dst[:, 16:32, :], in_=src[:, 16:32, ::-1]), [key], ["TP"])

            for bq in range(4):
                eng = ["dve", "act", "pool", "dve"][bq]
                src = Zc[:, bq * 8:(bq + 1) * 8, :].rearrange("p t (g c) -> p g t c", c=16)
                dst = Zc2[:, bq, :, :].rearrange("p g (t c) -> p g t c", c=16)
                if eng == "act":
                    P.op("act", lambda e, src=src, dst=dst: e.activation(out=dst, in_=src, func=AF.Copy), reads=["Zc"], writes=["SS"])
                else:
                    P.op(eng, lambda e, src=src, dst=dst: e.tensor_copy(out=dst, in_=src), reads=["Zc"], writes=["SS"])
            for g in range(32):
                for bq in range(4):
                    P.op("pe", lambda e, g=g, bq=bq: e.matmul(ps[:, 7, bq * 128:(bq + 1) * 128], lhsT=Zc2[:, bq, g, :], rhs=ident[:],
                                                                start=True, stop=True), reads=["SS", "ident"], writes=[bk(7)])
                P.op("act" if g % 2 else "dve", (lambda e, g=g: e.activation(out=U[:, g, :, :].rearrange("p b j -> p (b j)"), in_=bank(7), func=AF.Copy)) if g % 2 else
                     (lambda e, g=g: e.tensor_copy(out=U[:, g, :, :].rearrange("p b j -> p (b j)"), in_=bank(7))), reads=[bk(7)], writes=["U%d" % g])

            xgr = [T([128, 32, 16], BF16) for _ in range(2)]; xgi = [T([128, 32, 16], BF16) for _ in range(2)]
            Gr = [T([128, 4, 128], BF16) for _ in range(2)]; Gi = [T([128, 4, 128], BF16) for _ in range(2)]
            g1 = T([128, 32, 16]); g2 = T([128, 32, 16])
            V(lambda e: e.memset(SSraw[:], 0.0), [], ["SS"])
            nu = 0
            for gl in range(16):
                for dr in range(2):
                    un = dr * 16 + gl
                    b = nu % 2
                    nu += 1
                    st0 = 31 if dr == 0 else 32
                    tpr = TPr[:, un, st0 - 31:st0 + 1][:, ::-1].unsqueeze(2).broadcast_to([128, 32, 16])
                    tpi = TPi[:, un, st0 - 31:st0 + 1][:, ::-1].unsqueeze(2).broadcast_to([128, 32, 16])
                    bbr = bbre[:, un, :].unsqueeze(1).broadcast_to([128, 32, 16])
                    bbi = bbim[:, un, :].unsqueeze(1).broadcast_to([128, 32, 16])
                    tt(g1[:], tpr, bbr, ALU.mult, ["TP", "bbre"], ["g1"])
                    tt(g2[:], tpi, bbi, ALU.mult, ["TP", "bbim"], ["g2"])
                    tt(xgr[b][:], g1[:], g2[:], ALU.subtract, ["g1", "g2"], ["xgr%d" % b])
                    tt(g1[:], tpr, bbi, ALU.mult, ["TP", "bbim", "xgr%d" % b], ["g1"])
                    tt(g2[:], tpi, bbr, ALU.mult, ["TP", "bbre", "xgr%d" % b], ["g2"])
                    tt(xgi[b][:], g1[:], g2[:], ALU.add, ["g1", "g2"], ["xgi%d" % b])
                    for (xg, G, bankno, key) in [(xgr, Gr, 5, "xgr"), (xgi, Gi, 6, "xgi")]:
                        for bq in range(4):
                            P.op("pe", lambda e, xg=xg, b=b, bq=bq, bankno=bankno: e.matmul(ps[:, bankno, bq * 128:(bq + 1) * 128],
                                                                                              lhsT=xg[b][:, bq * 8:(bq + 1) * 8, :], rhs=ident[:], start=True, stop=True),
                                 reads=[key + "%d" % b, "ident"], writes=[bk(bankno)])
                        P.op("act", lambda e, G=G, b=b, bankno=bankno: e.activation(out=G[b][:].rearrange("p a n -> p (a n)"), in_=bank(bankno), func=AF.Copy),
                             reads=[bk(bankno)], writes=[key + "G%d" % b])
                    for (G, ri, bankno, key) in [(Gr, 0, 3, "xgr"), (Gi, 1, 4, "xgi")]:
                        for gh in range(2):
                            g = gh * 16 + gl
                            for bq in range(4):
                                P.op("pe", lambda e, G=G, b=b, gh=gh, g=g, bq=bq, bankno=bankno: e.matmul(
                                    ps[gh * 64:(gh + 1) * 64, bankno, 0:128], lhsT=G[b][:, bq, gh * 64:(gh + 1) * 64], rhs=U[:, g, bq, :],
                                    start=(bq == 0), stop=(bq == 3)), reads=[key + "G%d" % b, "U%d" % g], writes=[bk(bankno)])
                        lo = 1 if dr == 0 else 0
                        P.op("dve", lambda e, ri=ri, dr=dr, gl=gl, lo=lo, bankno=bankno: e.tensor_copy(out=SS[:, dr, ri, gl, lo:lo + 128], in_=ps[:, bankno, 0:128]),
                             reads=[bk(bankno)], writes=["SS"])
            LL = T([128, 2, 2, 16]); LM = T([128, 2, 2, 16])
            for dr in range(2):
                ix = 32 if dr == 0 else 0
                us = slice(dr * 16, dr * 16 + 16)
                for ri in range(2):
                    V(lambda e, dr=dr, ri=ri, ix=ix, us=us: e.tensor_copy(out=LL[:, dr, ri, :], in_=TPr[:, us, ix]), ["TP"], ["LL"])
                V(lambda e, dr=dr, ix=ix, us=us: e.tensor_scalar(out=LM[:, dr, 0, :], in0=TPi[:, us, ix], scalar1=-1.0, scalar2=None, op0=ALU.mult), ["TP"], ["LL"])
                V(lambda e, dr=dr, ix=ix, us=us: e.tensor_copy(out=LM[:, dr, 1, :], in_=TPi[:, us, ix]), ["TP"], ["LL"])
            s1 = T([128, 2, 16]); s2 = T([128, 2, 16])

            nsw = [0]

            def scan_dir(dr):
                order = range(0, 128) if dr == 0 else range(127, -1, -1)
                for j in order:
                    prv, cur = (j, j + 1) if dr == 0 else (j + 1, j)
                    if not (dr == 0 and j == 0):
                        sw_prev = swp[(nsw[0] - 1) % 2]
                        tt(s1[:], LL[:, dr, :, :], SS[:, dr, :, :, prv], ALU.mult, ["LL", "SS"], ["s1"])
                        tt(s2[:], LM[:, dr, :, :], sw_prev[:], ALU.mult, ["LL", "swp%d" % ((nsw[0] - 1) % 2)], ["s2"])
                        tt(SS[:, dr, :, :, cur], SS[:, dr, :, :, cur], s1[:], ALU.add, ["SS", "s1"], ["SS"])
                        tt(SS[:, dr, :, :, cur], SS[:, dr, :, :, cur], s2[:], ALU.add, ["SS", "s2"], ["SS"])
                    sw_cur = swp[nsw[0] % 2]
                    V(lambda e, dr=dr, cur=cur, sw_cur=sw_cur: e.tensor_copy(out=sw_cur[:], in_=SS[:, dr, ::-1, :, cur]), ["SS"], ["swp%d" % (nsw[0] % 2)])
                    nsw[0] += 1

            scan_dir(0)
            finS = T([128, 2, 16])
            V(lambda e: e.tensor_copy(out=finS[:], in_=SS[:, 0, :, :, 128]), ["SS"], ["finS"])
            P.op("sp", lambda e: e.dma_start(out=fin_in.ap(), in_=finS[:].rearrange("p a b -> p (a b)")), reads=["finS"], writes=["d_fin"], dma="fin")
            P.op("pool", lambda e: e.collective_compute("AllGather", ALU.bypass, replica_groups=[list(range(NCORES))],
                                                          ins=[fin_in.ap().opt()], outs=[fin_all.ap().opt()]),
                 reads=["d_fin"], writes=["d_finall"], dma="cc", inc=1)
            P.op("sp", lambda e: e.dma_start(out=finb[:], in_=fin_all.ap().rearrange("(r p) f -> p r f", p=128)), reads=["d_finall"], writes=["finb"], dma="finb")
            sin_ = T([128, 32])
            V(lambda e: e.memset(sin_[:], 0.0), [], ["sin"])
            for r in range(NCORES):
                V(lambda e, r=r: e.scalar_tensor_tensor(out=sin_[:], in0=finb[:, r, :], scalar=sel[:, r:r + 1], in1=sin_[:], op0=ALU.mult, op1=ALU.add),
                  ["finb", "sel", "sin"], ["sin"])
            V(lambda e: e.tensor_copy(out=SS[:, 1, :, :, 128], in_=sin_[:].rearrange("p (a b) -> p a b", a=2)), ["sin", "SS"], ["SS"])
            V(lambda e: e.tensor_copy(out=swp[nsw[0] % 2][:], in_=SS[:, 1, ::-1, :, 128]), ["SS"], ["swp%d" % (nsw[0] % 2)])
            nsw[0] += 1
            scan_dir(1)
            V(lambda e: e.tensor_copy(out=SSb[:].rearrange("p a b c d -> p (a b c d)"), in_=SSraw[:]), ["SS"], ["SSb"])
            if DEBUG:
                P.op("sp", lambda e: e.dma_start(out=dbg_ss.ap(), in_=SSraw[:]), reads=["SS"], writes=["d_dbgss"], dma="dbgss")

            Yr = [T([128, 33, 16], BF16) for _ in range(2)]; Yi = [T([128, 33, 16], BF16) for _ in range(2)]
            Xr = [T([128, 8, 16], BF16) for _ in range(2)]; Xi = [T([128, 8, 16], BF16) for _ in range(2)]
            Mc = [[T([128, 512], BF16) for _ in range(2)] for _ in range(2)]
            y1 = T([128, 33, 16]); y2 = T([128, 33, 16]); x1 = T([128, 8, 16]); x2 = T([128, 8, 16])
            ybuf = [T([128, 32, 16]) for _ in range(2)]
            ytmp = [T([128, 32, 16]) for _ in range(2)]
            for gl in range(16):
                for dr in range(2):
                    un = dr * 16 + gl
                    tpr = TPr[:, un, :].unsqueeze(2).broadcast_to([128, 33, 16]); tpi = TPi[:, un, :].unsqueeze(2).broadcast_to([128, 33, 16])
                    cr = Cre[:, un, :].unsqueeze(1).broadcast_to([128, 33, 16]); ci = Cim[:, un, :].unsqueeze(1).broadcast_to([128, 33, 16])
                    tt(y1[:], tpr, cr, ALU.mult, ["TP", "sp_t"], ["y1"])
                    tt(y2[:], tpi, ci, ALU.mult, ["TP", "sp_t"], ["y2"])
                    tt(Yr[dr][:], y1[:], y2[:], ALU.subtract, ["y1", "y2"], ["Yr%d" % dr])
                    tt(y1[:], tpr, ci, ALU.mult, ["TP", "sp_t", "Yr%d" % dr], ["y1"])
                    tt(y2[:], tpi, cr, ALU.mult, ["TP", "sp_t", "Yr%d" % dr], ["y2"])
                    V(lambda e, dr=dr: e.scalar_tensor_tensor(out=Yi[dr][:].rearrange("p a b -> p (a b)"), in0=y1[:].rearrange("p a b -> p (a b)"), scalar=-1.0,
                                                              in1=y2[:].rearrange("p a b -> p (a b)"), op0=ALU.mult, op1=ALU.subtract), ["y1", "y2"], ["Yi%d" % dr])
                    tnr = TNr[:, un, :].unsqueeze(2).broadcast_to([128, 8, 16]); tni = TNi[:, un, :].unsqueeze(2).broadcast_to([128, 8, 16])
                    br_ = bbre[:, un, :].unsqueeze(1).broadcast_to([128, 8, 16]); bi_ = bbim[:, un, :].unsqueeze(1).broadcast_to([128, 8, 16])
                    tt(x1[:], tnr, br_, ALU.mult, ["TP", "bbre"], ["x1"])
                    tt(x2[:], tni, bi_, ALU.mult, ["TP", "bbim"], ["x2"])
                    tt(Xr[dr][:], x1[:], x2[:], ALU.subtract, ["x1", "x2"], ["Xr%d" % dr])
                    tt(x1[:], tnr, bi_, ALU.mult, ["TP", "bbim", "Xr%d" % dr], ["x1"])
                    tt(x2[:], tni, br_, ALU.mult, ["TP", "bbre", "Xr%d" % dr], ["x2"])
                    tt(Xi[dr][:], x1[:], x2[:], ALU.add, ["x1", "x2"], ["Xi%d" % dr])
                    ys = 0 if dr == 0 else 1
                    for gh in range(2):
                        rows = slice(gh * 64, gh * 64 + 64)
                        bankno = 5 + gh
                        P.op("pe", lambda e, dr=dr, rows=rows, ys=ys, bankno=bankno: e.matmul(bank(bankno), lhsT=Xr[dr][rows, :, :], rhs=Yr[dr][rows, ys:ys + 32, :], start=True, stop=False),
                             reads=["Xr%d" % dr, "Yr%d" % dr], writes=[bk(bankno)])
                        P.op("pe", lambda e, dr=dr, rows=rows, ys=ys, bankno=bankno: e.matmul(bank(bankno), lhsT=Xi[dr][rows, :, :], rhs=Yi[dr][rows, ys:ys + 32, :], start=False, stop=True),
                             reads=["Xi%d" % dr, "Yi%d" % dr], writes=[bk(bankno)])
                        msk = maskA if dr == 0 else maskB
                        P.op("dve", lambda e, dr=dr, gh=gh, bankno=bankno, msk=msk: e.tensor_tensor(out=Mc[dr][gh][:], in0=bank(bankno), in1=msk[:], op=ALU.mult),
                             reads=[bk(bankno), "maskA", "maskB"], writes=["Mc%d%d" % (dr, gh)])
                for gh in range(2):
                    g = gh * 16 + gl
                    rows = slice(gh * 64, gh * 64 + 64)
                    bankno = 1 + (g % 2)
                    first = [True]

                    def mm(lhsT, rhs, outap, rd):
                        st = first[0]
                        first[0] = False
                        P.op("pe", lambda e, lhsT=lhsT, rhs=rhs, outap=outap, st=st: e.matmul(outap, lhsT=lhsT, rhs=rhs, start=st, stop=False), reads=rd, writes=[bk(bankno)])

                    for bq in range(4):
                        mm(U[:, g, bq, :], Mc[0][gh][:, 0:(4 - bq) * 128], ps[:, bankno, bq * 128:512], ["U%d" % g, "Mc0%d" % gh])
                    for bq in range(4):
                        mm(U[:, g, bq, :], Mc[1][gh][:, (3 - bq) * 128:512], ps[:, bankno, 0:(bq + 1) * 128], ["U%d" % g, "Mc1%d" % gh])
                    for dr in range(2):
                        lo = 0 if dr == 0 else 1
                        hs = 1 if dr == 0 else 0
                        mm(SSb[rows, dr, 0, gl, lo:lo + 128], Yr[dr][rows, hs:hs + 32, :], bank(bankno), ["SSb", "Yr%d" % dr])
                        mm(SSb[rows, dr, 1, gl, lo:lo + 128], Yi[dr][rows, hs:hs + 32, :], bank(bankno), ["SSb", "Yi%d" % dr])
                    yb = ybuf[g % 2]; yt = ytmp[g % 2]
                    zsl = Zc[:, :, g * 16:(g + 1) * 16]
                    dsl = dtab[:, g * 16:(g + 1) * 16].unsqueeze(1).broadcast_to([128, 32, 16])
                    tt(yt[:], zsl, dsl, ALU.mult, ["Zc%d" % g, "Zc", "dtab"], ["yt%d" % (g % 2)])
                    V(lambda e, yb=yb, yt=yt, bankno=bankno: e.tensor_tensor(out=yb[:].rearrange("p a b -> p (a b)"), in0=bank(bankno), in1=yt[:].rearrange("p a b -> p (a b)"), op=ALU.add),
                      [bk(bankno), "yt%d" % (g % 2)], ["yb%d" % (g % 2)])
                    V(lambda e, yb=yb, zsl=zsl: e.activation(out=zsl, in_=yb[:], func=AF.Gelu_apprx_tanh), ["yb%d" % (g % 2)], ["Zc%d" % g], "act")
            ysv = ysT.rearrange("p c (j t) -> p c t j", t=32)
            nb = 0
            for cb in range(4):
                for t4 in range(8):
                    bankno = 3 + (nb % 2)
                    nb += 1
                    for ti in range(4):
                        t = t4 * 4 + ti
                        P.op("pe", lambda e, cb=cb, t=t, ti=ti, bankno=bankno: e.matmul(ps[:, bankno, ti * 128:(ti + 1) * 128], lhsT=Zc[:, t, cb * 128:(cb + 1) * 128], rhs=ident[:],
                                                                                      start=True, stop=True),
                             reads=["Zc"] + ["Zc%d" % g for g in range(cb * 8, cb * 8 + 8)] + ["ident"], writes=[bk(bankno)])
                    eng = "act" if nb % 2 else "dve"
                    if eng == "act":
                        P.op("act", lambda e, cb=cb, t4=t4, bankno=bankno: e.activation(out=ysv[:, cb, t4 * 4:(t4 + 1) * 4, :], in_=bank(bankno).rearrange("p (a j) -> p a j", a=4), func=AF.Copy),
                             reads=[bk(bankno)], writes=["ysT"])
                    else:
                        P.op("dve", lambda e, cb=cb, t4=t4, bankno=bankno: e.tensor_copy(out=ysv[:, cb, t4 * 4:(t4 + 1) * 4, :], in_=bank(bankno).rearrange("p (a j) -> p a j", a=4)),
                             reads=[bk(bankno)], writes=["ysT"])
            if DEBUG:
                P.op("sp", lambda e: e.dma_start(out=dbg_ys.ap().rearrange("c p t -> p c t"), in_=ysT), reads=["ysT"], writes=["d_dbgys"], dma="dbgys")
            P.emit_phase("phaseS")
        P.stack = outer

        if STOP <= 2:
            return nc
        with ExitStack() as ph2:
            P.stack = ph2
            h = P.sbuf("h2", [128, 8, 1024], F32)
            btabA = P.sbuf("btabA", [128, 40, 128], BF16)
            wglu = P.sbuf("wglu", [128, 4, 512], BF16)
            wbs = P.sbuf("wbs", [128, 4, 1024], BF16)
            wba = P.sbuf("wba", [128, 4, 1024], BF16)
            wo = P.sbuf("wo", [128, 8, 1024], BF16)
            bglu = P.sbuf("bglu", [128, 4], F32)
            rcp = P.sbuf("rcp", [128, 8, 1], F32)
            first_emit = [True]

            def load_persist():
                P.op("pool", lambda e: e.dma_start(out=btabA[:], in_=btab_d[:, 64:104, :], max_dma_last_dim=2048), writes=["btabA"], dma="btabA")
                P.op("pool", lambda e: e.dma_start(out=wglu[:], in_=W["ssm_w_glu"].ap().rearrange("(c p) f -> p c f", p=128)), writes=["wglu"], dma="wglu")
                P.op("pool", lambda e: e.dma_start(out=wbs[:], in_=W["w_branch_ssm"].ap().rearrange("(c p) f -> p c f", p=128)), writes=["wbs"], dma="wbs")
                P.op("pool", lambda e: e.dma_start(out=wba[:], in_=W["w_branch_att"].ap().rearrange("(c p) f -> p c f", p=128)), writes=["wba"], dma="wba")
                P.op("pool", lambda e: e.dma_start(out=wo[:], in_=W["w_out"].ap().rearrange("(c p) f -> p c f", p=128)), writes=["wo"], dma="wo")
                P.op("sp", lambda e: e.dma_start(out=bglu[:], in_=bglu_d.ap()), writes=["bglu"], dma="bglu")

            for gi in range(4):
                with ExitStack() as stm:
                    P.stack = stm
                    qt = P.sbuf("qt%d" % gi, [128, 4, 512], BF16)
                    kt = P.sbuf("kt%d" % gi, [128, 4, 1024], BF16)
                    vw = P.sbuf("vw%d" % gi, [128, 8, 520], BF16)
                    sgs = P.sbuf("sgs%d" % gi, [128, 8, 512], BF16)
                    sga = P.sbuf("sga%d" % gi, [128, 8, 512], BF16)
                    h1t = P.sbuf("h1t%d" % gi, [128, 8, 512], F32)
                    sig = [P.sbuf("sig%d_%d" % (i, gi), [128, 512], BF16) for i in range(2)]
                    y2t = P.sbuf("y2t%d" % gi, [128, 4, 512], BF16)
                    pT = [P.sbuf("pT%d_%d" % (i, gi), [128, 5, 128], BF16) for i in range(2)]
                    ya = [P.sbuf("ya%d_%d" % (i, gi), [128, 8, 64], BF16) for i in range(2)]
                    yaT = P.sbuf("yaT%d" % gi, [128, 4, 512], BF16)
                    m1 = [P.sbuf("m1_%d_%d" % (i, gi), [128, 512], F32) for i in range(1)] * 2
                    m2 = [P.sbuf("m2_%d_%d" % (i, gi), [128, 512], F32) for i in range(1)] * 2
                    mrg = P.sbuf("mrg%d" % gi, [128, 8, 512], BF16)
                    if gi == 0:
                        btabB = P.sbuf("btabB", [128, 64, 128], BF16)
                        load_persist()
                        P.op("pool", lambda e: e.dma_start(out=btabB[:], in_=btab_d[:, 0:64, :], max_dma_last_dim=2048), writes=["btabB"], dma="btabB")
                    for s in range(2):
                        k = gi * 2 + s
                        ce = HALO + 512 * k
                        co = 512 * k
                        hsl = slice(s * 512, s * 512 + 512)
                        P.op("sp", lambda e, ce=ce: e.dma_start(out=qt[:], in_=qT_s.ap().rearrange("c p t -> p c t")[:, :, ce:ce + 512]), writes=["qt"], dma="qt")
                        P.op("sp", lambda e, k=k: e.dma_start(out=kt[:], in_=kT_s.ap().rearrange("c p t -> p c t")[:, :, 512 * k:512 * k + 1024]), writes=["kt"], dma="kt")
                        P.op("sp", lambda e, k=k: e.dma_start(out=vw[:], in_=va_s[512 * k:512 * k + 1024, :].rearrange("(a p) f -> p a f", p=128)), writes=["vw"], dma="vw")
                        P.op("act", lambda e, ce=ce: e.dma_start(out=sgs[:], in_=sgs_s.ap().rearrange("c p t -> p c t")[:, :, ce:ce + 512]), writes=["sgs"], dma="sgs")
                        P.op("act", lambda e, ce=ce: e.dma_start(out=sga[:], in_=sga_s.ap().rearrange("c p t -> p c t")[:, :, ce:ce + 512]), writes=["sga"], dma="sga")
                        P.op("sp", lambda e, ce=ce: e.dma_start(out=h1t[:], in_=h1_s.ap().rearrange("c p t -> p c t")[:, :, ce:ce + 512]), writes=["h1t"], dma="h1t")
                        for ob in range(4):
                            for cb in range(4):
                                P.op("pe", lambda e, ob=ob, cb=cb, co=co: e.matmul(bank(7), lhsT=wglu[:, cb, ob * 128:(ob + 1) * 128], rhs=ysT[:, cb, co:co + 512], start=(cb == 0), stop=(cb == 3)),
                                     reads=["wglu", "ysT"], writes=[bk(7)])
                            sb_ = ob % 2
                            P.op("act", lambda e, ob=ob, sb_=sb_: e.activation(out=sig[sb_][:], in_=bank(7), func=AF.Sigmoid, bias=bglu[:, ob:ob + 1]),
                                 reads=[bk(7), "bglu"], writes=["sig%d" % sb_])
                            P.op("dve", lambda e, ob=ob, sb_=sb_, co=co: e.tensor_tensor(out=y2t[:, ob, :], in0=ysT[:, ob, co:co + 512], in1=sig[sb_][:], op=ALU.mult),
                                 reads=["sig%d" % sb_, "ysT"], writes=["y2t"])
                        def pair_cfg(qi, k=k):
                            m = 4 * k + qi
                            if m == 0:
                                return [0, 1, 2, 3], 0, btabB, "btabB"
                            if m == 1:
                                return [-1, 0, 1, 2], 4, btabB, "btabB"
                            return [-2, -1, 0, 1, 2], 0, btabA, "btabA"

                        def emit_qk(qi, hh):
                            offs, tb0, btab, bkey = pair_cfg(qi)
                            no = len(offs)
                            qsl = slice(qi * 128, qi * 128 + 128)
                            hb = hh // 2
                            rows = slice((hh % 2) * 64, (hh % 2) * 64 + 64)
                            pb = hh % 2
                            b0 = 1 + 2 * pb
                            for oi, o in enumerate(offs):
                                wi = qi + o + 2
                                oap = psflat[:, b0 * 512 + oi * 128: b0 * 512 + (oi + 1) * 128]
                                P.op("pe", lambda e, rows=rows, hb=hb, wi=wi, qsl=qsl, oap=oap: e.matmul(oap, lhsT=kt[rows, hb, wi * 128:(wi + 1) * 128], rhs=qt[rows, hb, qsl], start=True, stop=False),
                                     reads=["kt", "qt"], writes=[bk(b0), bk(b0 + 1)])
                                P.op("pe", lambda e, oap=oap, tb0=tb0, oi=oi, hh=hh, btab=btab: e.matmul(oap, lhsT=ident[:], rhs=btab[:, (tb0 + oi) * 8 + hh, :], start=False, stop=True),
                                     reads=[bkey, "ident"], writes=[bk(b0), bk(b0 + 1)])
                            P.op("act", lambda e, pb=pb, b0=b0: e.activation(out=pT[pb][:, 0:4, :].rearrange("p a b -> p (a b)"), in_=psflat[:, b0 * 512:b0 * 512 + 512], func=AF.Exp),
                                 reads=[bk(b0), bk(b0 + 1)], writes=["pT%d" % pb])
                            if no == 5:
                                P.op("act", lambda e, pb=pb, b0=b0: e.activation(out=pT[pb][:, 4, :], in_=psflat[:, (b0 + 1) * 512:(b0 + 1) * 512 + 128], func=AF.Exp),
                                     reads=[bk(b0), bk(b0 + 1)], writes=["pT%d" % pb])

                        def emit_pv(qi, hh):
                            offs, tb0, btab, bkey = pair_cfg(qi)
                            no = len(offs)
                            pb = hh % 2
                            ob_ = 5 + hh // 4
                            for oi, o in enumerate(offs):
                                wi = qi + o + 2
                                P.op("pe", lambda e, pb=pb, oi=oi, wi=wi, hh=hh, ob_=ob_, no=no: e.matmul(ps[:, ob_, (hh % 4) * 65:(hh % 4) * 65 + 65], lhsT=pT[pb][:, oi, :], rhs=vw[:, wi, hh * 65:(hh + 1) * 65],
                                                                                                        start=(oi == 0), stop=(oi == no - 1)),
                                     reads=["pT%d" % pb, "vw"], writes=[bk(ob_)])

                        def emit_fin(qi):
                            qsl = slice(qi * 128, qi * 128 + 128)
                            yb_ = qi % 2
                            for hf in range(2):
                                pv = ps[:, 5 + hf, 0:260].rearrange("p (h d) -> p h d", d=65)
                                P.op("dve", lambda e, hf=hf, pv=pv: e.reciprocal(out=rcp[:, hf * 4:(hf + 1) * 4, :], in_=pv[:, :, 64:65]), reads=[bk(5 + hf)], writes=["rcp%d" % hf])
                                P.op("dve", lambda e, hf=hf, pv=pv, yb_=yb_: e.tensor_tensor(out=ya[yb_][:, hf * 4:(hf + 1) * 4, :], in0=pv[:, :, 0:64],
                                                                                             in1=rcp[:, hf * 4:(hf + 1) * 4, :].broadcast_to([128, 4, 64]), op=ALU.mult),
                                     reads=[bk(5 + hf), "rcp%d" % hf], writes=["ya%d_%d" % (yb_, hf)])
                            yaf = ya[yb_][:].rearrange("p h d -> p (h d)")
                            for hb in range(4):
                                P.op("pe", lambda e, hb=hb, yaf=yaf: e.matmul(ps[:, 0, hb * 128:(hb + 1) * 128], lhsT=yaf[:, hb * 128:(hb + 1) * 128], rhs=ident[:], start=True, stop=True),
                                     reads=["ya%d_0" % yb_, "ya%d_1" % yb_, "ident"], writes=[bk(0)])
                            P.op("act", lambda e, qsl=qsl: e.activation(out=yaT[:, :, qsl], in_=bank(0).rearrange("p (a j) -> p a j", a=4), func=AF.Copy), reads=[bk(0)], writes=["yaT"])

                        items = [(qi, hh) for qi in range(4) for hh in range(8)]
                        emit_qk(*items[0])
                        for ii, (qi, hh) in enumerate(items):
                            if ii + 1 < len(items):
                                emit_qk(*items[ii + 1])
                            emit_pv(qi, hh)
                            if hh == 7:
                                emit_fin(qi)
                        if DEBUG:
                            P.op("sp", lambda e, co=co: e.dma_start(out=dbg_ya.ap().rearrange("c p t -> p c t")[:, :, co:co + 512], in_=yaT[:]), reads=["yaT"], writes=["d_dbgya"], dma="dbgya")
                        for ob in range(8):
                            mb = 0
                            for cb in range(4):
                                P.op("pe", lambda e, ob=ob, cb=cb: e.matmul(bank(7), lhsT=wbs[:, cb, ob * 128:(ob + 1) * 128], rhs=y2t[:, cb, :], start=(cb == 0), stop=(cb == 3)),
                                     reads=["wbs", "y2t"], writes=[bk(7)])
                            P.op("dve", lambda e, ob=ob, mb=mb: e.tensor_tensor(out=m1[mb][:], in0=bank(7), in1=sgs[:, ob, :], op=ALU.mult), reads=[bk(7), "sgs"], writes=["m1%d" % mb])
                            for cb in range(4):
                                P.op("pe", lambda e, ob=ob, cb=cb: e.matmul(bank(0), lhsT=wba[:, cb, ob * 128:(ob + 1) * 128], rhs=yaT[:, cb, :], start=(cb == 0), stop=(cb == 3)),
                                     reads=["wba", "yaT"], writes=[bk(0)])
                            P.op("dve", lambda e, ob=ob, mb=mb: e.tensor_tensor(out=m2[mb][:], in0=bank(0), in1=sga[:, ob, :], op=ALU.mult), reads=[bk(0), "sga"], writes=["m2%d" % mb])
                            P.op("dve", lambda e, ob=ob, mb=mb: e.tensor_tensor(out=mrg[:, ob, :], in0=m1[mb][:], in1=m2[mb][:], op=ALU.add), reads=["m1%d" % mb, "m2%d" % mb], writes=["mrg"])
                        for ob in range(8):
                            bn = 7 if ob % 2 else 0
                            for c in range(8):
                                P.op("pe", lambda e, ob=ob, c=c, bn=bn: e.matmul(bank(bn), lhsT=wo[:, c, ob * 128:(ob + 1) * 128], rhs=mrg[:, c, :], start=(c == 0), stop=(c == 7)),
                                     reads=["wo", "mrg"], writes=[bk(bn)])
                            P.op("dve", lambda e, ob=ob, bn=bn, hsl=hsl: e.tensor_tensor(out=h[:, ob, hsl], in0=bank(bn), in1=h1t[:, ob, :], op=ALU.add), reads=[bk(bn), "h1t"], writes=["h"])
                        if DEBUG:
                            P.op("sp", lambda e, co=co, hsl=hsl: e.dma_start(out=dbg_h2.ap().rearrange("c p t -> p c t")[:, :, co:co + 512], in_=h[:, :, hsl]), reads=["h"], writes=["d_dbgh2"], dma="dbgh2")
                    P.emit_phase("mix%d" % gi)
                with ExitStack() as stf:
                    P.stack = stf
                    u = P.sbuf("u2_%d" % gi, [128, 8, 1024], BF16)
                    act = P.sbuf("act2_%d" % gi, [128, NF, 1024], BF16)
                    sq = P.sbuf("sq2_%d" % gi, [128, 8, 512], BF16)
                    sd = P.sbuf("sd2_%d" % gi, [128, 512], F32)
                    rstd = P.sbuf("rstd2_%d" % gi, [128, 512], F32)
                    wg2 = [P.sbuf("wg2b_%d_%d" % (i, gi), [128, 8, 128], BF16) for i in range(2)]
                    wu2 = [P.sbuf("wu2b_%d_%d" % (i, gi), [128, 8, 128], BF16) for i in range(2)]
                    wdq = [P.sbuf("wdqb_%d_%d" % (i, gi), [128, NF, 128], BF16) for i in range(2)]
                    sgb = [P.sbuf("sgbb_%d_%d" % (i, gi), [128, 512], BF16) for i in range(2)]
                    ost = [P.sbuf("ost%d_%d" % (i, gi), [128, 512], F32) for i in range(2)]
                    for s in range(2):
                        sl = rmsnorm(h, u, sq, sd, rstd, 2, s)
                        apply_norm(h, u, rstd, 2, sl)
                    ffn(h, u, act, W["ffn2_w_gate"], W["ffn2_w_up"], W["ffn2_w_down"], 2, (wg2, wu2, wdq, sgb))
                    for s in range(2):
                        sl = rmsnorm(h, u, sq, sd, rstd, 3, s)
                        co = gi * 1024 + s * 512
                        for c in range(8):
                            ob_ = c % 2
                            P.op("dve", lambda e, c=c, sl=sl, ob_=ob_: e.scalar_tensor_tensor(out=ost[ob_][:], in0=h[:, c, sl], scalar=gains[:, 3, c:c + 1], in1=rstd[:], op0=ALU.mult, op1=ALU.mult),
                                 reads=["h", "rstd", "gains"], writes=["ost%d" % ob_])
                            P.op("sp", lambda e, co=co, c=c, ob_=ob_: e.dma_start(out=outT[c * 128:(c + 1) * 128, co:co + 512], in_=ost[ob_][:]), reads=["ost%d" % ob_], writes=["d_out"], dma="outo%d" % ob_)
                    P.emit_phase("ffn%d" % gi)
            P.stack = ph2
        P.stack = outer
    return nc


_CACHE = {}


def kernel(**inputs):
    inp = {k: np.asarray(v) for k, v in inputs.items()}
    if "nc" not in _CACHE:
        _CACHE["nc"] = build()
    nc = _CACHE["nc"]
    consts = _consts()
    in_maps = []
    for c in range(NCORES):
        m = _prep_core(c, inp)
        m.update(consts)
        for nm in SHARED:
            m[nm] = np.ascontiguousarray(inp[nm][0])
        in_maps.append(m)
    res = run_bass_kernel_spmd(nc, in_maps, core_ids=list(range(NCORES)))
    out = np.zeros((4, SEQ, D), np.float32)
    for c in range(NCORES):
        o = np.asarray(res.results[c]["outT"]).T
        b = c // 2
        if c % 2 == 0:
            out[b, 0:TOK] = o
        else:
            out[b, TOK:SEQ] = o[::-1]
    if DEBUG:
        _CACHE["res"] = res
    return out
```
